# Optimizing a Trainium2 kernel written in Bass

```python
import math
import jax, jax.numpy as jnp
from jax import lax
import numpy as np

D_MODEL = 2048
BATCH = 8
SEQ = 2048
DEPTH = 2

CHUNK = 64
Q_BLOCK = 128
N_MIXERS = 2
N_ATTN = (DEPTH + 1) // 2
N_RWKV = DEPTH // 2

DA_HEADS = 8
DA_HEAD_DIM = D_MODEL // DA_HEADS // 2
DA_V_DIM = 2 * DA_HEAD_DIM
ROT_DIM = DA_HEAD_DIM // 4
ROPE_THETA = 500000.0
DA_SUBLN_EPS = 1e-5

RW_HEAD = 64
RW_HEADS = D_MODEL // RW_HEAD
LORA_DECAY = 96
LORA_A = 96
LORA_GATE = 256
GN_EPS = 64e-5

FFN = 4 * D_MODEL
EPS = 1e-6
NEG_INF = -1e30

kernel_name = "hybrid_diffattn_rwkv7_sqrelu_stream"


def rms_norm(x, g, eps=EPS):
    xf = x.astype(jnp.float32)
    y = xf * lax.rsqrt(jnp.mean(xf * xf, axis=-1, keepdims=True) + eps)
    return (y * g.astype(jnp.float32)).astype(x.dtype)


def rope_partial(x, pos):
    half = ROT_DIM // 2
    inv_freq = ROPE_THETA ** (-jnp.arange(half, dtype=jnp.float32) * (2.0 / ROT_DIM))
    ang = pos[:, None] * inv_freq[None, :]
    cos = jnp.cos(ang)[None, :, None, :].astype(x.dtype)
    sin = jnp.sin(ang)[None, :, None, :].astype(x.dtype)
    x1 = x[..., :half]
    x2 = x[..., half:ROT_DIM]
    xp = x[..., ROT_DIM:]
    return jnp.concatenate([x1 * cos - x2 * sin, x2 * cos + x1 * sin, xp], axis=-1)


def diff_attention(h, wq, wk, wv, wo, lam_vecs, subln_g, lambda_init):
    B, S, _ = h.shape
    q = (h @ wq).reshape(B, S, 2 * DA_HEADS, DA_HEAD_DIM)
    k = (h @ wk).reshape(B, S, 2 * DA_HEADS, DA_HEAD_DIM)
    v = (h @ wv).reshape(B, S, DA_HEADS, DA_V_DIM)
    pos = jnp.arange(S, dtype=jnp.float32)
    q = rope_partial(q, pos) * (DA_HEAD_DIM ** -0.5)
    k = rope_partial(k, pos)
    lv = lam_vecs.astype(jnp.float32)
    lam = jnp.exp(jnp.sum(lv[0] * lv[1])) - jnp.exp(jnp.sum(lv[2] * lv[3])) + lambda_init
    outs = []
    for qb in range(S // Q_BLOCK):
        q0 = qb * Q_BLOCK
        k_end = q0 + Q_BLOCK
        s = jnp.einsum('bqhd,bkhd->bhqk', q[:, q0:k_end], k[:, :k_end]).astype(jnp.float32)
        q_chunk = (q0 + jnp.arange(Q_BLOCK)) // CHUNK
        k_chunk = jnp.arange(k_end) // CHUNK
        allowed = k_chunk[None, :] <= q_chunk[:, None]
        s = jnp.where(allowed[None, None], s, NEG_INF)
        p = jax.nn.softmax(s, axis=-1).reshape(B, DA_HEADS, 2, Q_BLOCK, k_end)
        pd = (p[:, :, 0] - lam * p[:, :, 1]).astype(v.dtype)
        outs.append(jnp.einsum('bhqk,bkhe->bqhe', pd, v[:, :k_end]))
    o = jnp.concatenate(outs, axis=1)
    o = rms_norm(o, subln_g, DA_SUBLN_EPS) * (1.0 - lambda_init)
    return o.reshape(B, S, DA_HEADS * DA_V_DIM) @ wo


def rwkv7_time_mix(h, mix, wr, wk, wv, wo, w0, w1, w2, a0, a1, a2, g1, g2,
                   k_k, k_a, r_k, lnx_g, lnx_b):
    B, S, D = h.shape
    f32 = jnp.float32
    xx = jnp.pad(h, ((0, 0), (1, 0), (0, 0)))[:, :-1] - h
    xr = h + xx * mix[0]
    xw = h + xx * mix[1]
    xk = h + xx * mix[2]
    xv = h + xx * mix[3]
    xa = h + xx * mix[4]
    xg = h + xx * mix[5]
    r = xr @ wr
    w = -jax.nn.softplus(-(w0 + jnp.tanh(xw @ w1) @ w2)) - 0.5
    k = xk @ wk
    v = xv @ wv
    a = jax.nn.sigmoid(a0 + (xa @ a1) @ a2)
    g = jax.nn.sigmoid(xg @ g1) @ g2

    def heads(t):
        return t.reshape(B, S, RW_HEADS, RW_HEAD).astype(f32)

    kk = heads(k * k_k)
    kk = kk / jnp.maximum(jnp.sqrt(jnp.sum(kk * kk, axis=-1, keepdims=True)), 1e-12)
    k = k * (1.0 + (a - 1.0) * k_a)
    r_h, k_h, v_h, a_h = heads(r), heads(k), heads(v), heads(a)
    decay = jnp.exp(-jnp.exp(heads(w)))

    def step(state, inp):
        rt, wt, kt, vt, kkt, at = inp
        sa = jnp.einsum('bhvk,bhk->bhv', state, -kkt)
        state = (state * wt[:, :, None, :]
                 + sa[..., None] * (kkt * at)[:, :, None, :]
                 + vt[..., None] * kt[:, :, None, :])
        yt = jnp.einsum('bhvk,bhk->bhv', state, rt)
        return state, yt

    xs = tuple(jnp.moveaxis(t, 1, 0) for t in (r_h, decay, k_h, v_h, kk, a_h))
    state0 = jnp.zeros((B, RW_HEADS, RW_HEAD, RW_HEAD), f32)
    _, y = lax.scan(step, state0, xs)
    y = jnp.moveaxis(y, 0, 1)
    mu = jnp.mean(y, axis=-1, keepdims=True)
    var = jnp.mean(jnp.square(y - mu), axis=-1, keepdims=True)
    y = ((y - mu) * lax.rsqrt(var + GN_EPS)).reshape(B, S, D)
    y = y * lnx_g.astype(f32) + lnx_b.astype(f32)
    bonus = jnp.sum(r_h * k_h * r_k.astype(f32), axis=-1, keepdims=True) * v_h
    out = (y + bonus.reshape(B, S, D)).astype(h.dtype) * g
    return out @ wo


def sqrelu_mlp(h, w1, w2):
    u = jax.nn.relu(h @ w1)
    return (u * u) @ w2


def setup_inputs(seed: int = 0) -> dict:
    key = jax.random.key(seed)
    ks = iter(jax.random.split(key, 48))
    f32 = jnp.float32

    def nrm(shape, scale):
        return jax.random.normal(next(ks), shape, f32) * scale

    def gain(shape):
        return 1.0 + 0.02 * jax.random.normal(next(ks), shape, f32)

    D = D_MODEL
    sd = D ** -0.5
    return {
        "x": nrm((BATCH, SEQ, D), 1.0),
        "g_pre_mix": gain((DEPTH, D)),
        "g_post_mix": gain((DEPTH, D)),
        "g_pre_ffn": gain((DEPTH, D)),
        "g_post_ffn": gain((DEPTH, D)),
        "ffn_w1": nrm((DEPTH, D, FFN), sd),
        "ffn_w2": nrm((DEPTH, FFN, D), FFN ** -0.5),
        "da_wq": nrm((N_ATTN, D, 2 * DA_HEADS * DA_HEAD_DIM), sd),
        "da_wk": nrm((N_ATTN, D, 2 * DA_HEADS * DA_HEAD_DIM), sd),
        "da_wv": nrm((N_ATTN, D, DA_HEADS * DA_V_DIM), sd),
        "da_wo": nrm((N_ATTN, DA_HEADS * DA_V_DIM, D), sd),
        "da_lambda": nrm((N_ATTN, 4, DA_HEAD_DIM), 0.1),
        "da_subln": gain((N_ATTN, DA_V_DIM)),
        "rw_mix": jax.random.uniform(next(ks), (N_RWKV, 6, D), f32),
        "rw_wr": nrm((N_RWKV, D, D), sd),
        "rw_wk": nrm((N_RWKV, D, D), sd),
        "rw_wv": nrm((N_RWKV, D, D), sd),
        "rw_wo": nrm((N_RWKV, D, D), sd),
        "rw_w0": jax.random.uniform(next(ks), (N_RWKV, D), f32, -6.0, -1.0),
        "rw_w1": nrm((N_RWKV, D, LORA_DECAY), sd),
        "rw_w2": nrm((N_RWKV, LORA_DECAY, D), 0.1 * LORA_DECAY ** -0.5),
        "rw_a0": nrm((N_RWKV, D), 0.1),
        "rw_a1": nrm((N_RWKV, D, LORA_A), sd),
        "rw_a2": nrm((N_RWKV, LORA_A, D), 0.1 * LORA_A ** -0.5),
        "rw_g1": nrm((N_RWKV, D, LORA_GATE), sd),
        "rw_g2": nrm((N_RWKV, LORA_GATE, D), LORA_GATE ** -0.5),
        "rw_kk": 0.85 + 0.05 * jax.random.normal(next(ks), (N_RWKV, D), f32),
        "rw_ka": gain((N_RWKV, D)),
        "rw_rk": nrm((N_RWKV, RW_HEADS, RW_HEAD), 0.1),
        "rw_lnx_g": gain((N_RWKV, D)),
        "rw_lnx_b": nrm((N_RWKV, D), 0.01),
    }


def reference(x, g_pre_mix, g_post_mix, g_pre_ffn, g_post_ffn, ffn_w1, ffn_w2,
              da_wq, da_wk, da_wv, da_wo, da_lambda, da_subln,
              rw_mix, rw_wr, rw_wk, rw_wv, rw_wo, rw_w0, rw_w1, rw_w2,
              rw_a0, rw_a1, rw_a2, rw_g1, rw_g2, rw_kk, rw_ka, rw_rk,
              rw_lnx_g, rw_lnx_b):
    for i in range(DEPTH):
        j = i // N_MIXERS
        hn = rms_norm(x, g_pre_mix[i])
        if i % N_MIXERS == 0:
            lambda_init = 0.8 - 0.6 * math.exp(-0.3 * i)
            m = diff_attention(hn, da_wq[j], da_wk[j], da_wv[j], da_wo[j],
                               da_lambda[j], da_subln[j], lambda_init)
        else:
            m = rwkv7_time_mix(hn, rw_mix[j], rw_wr[j], rw_wk[j], rw_wv[j], rw_wo[j],
                               rw_w0[j], rw_w1[j], rw_w2[j], rw_a0[j], rw_a1[j], rw_a2[j],
                               rw_g1[j], rw_g2[j], rw_kk[j], rw_ka[j], rw_rk[j],
                               rw_lnx_g[j], rw_lnx_b[j])
        x = x + rms_norm(m, g_post_mix[i])
        hn = rms_norm(x, g_pre_ffn[i])
        x = x + rms_norm(sqrelu_mlp(hn, ffn_w1[i], ffn_w2[i]), g_post_ffn[i])
    return x
```

```python
import numpy as np
import concourse.bass as bass
import concourse.mybir as mybir
from concourse.bass_utils import run_bass_kernel_spmd

F32 = mybir.dt.float32
BF16 = mybir.dt.bfloat16
AF = mybir.ActivationFunctionType
ALU = mybir.AluOpType
AX = mybir.AxisListType

D = 2048
DC = 16
FFN = 8192
P = 128


class Tile:
    __slots__ = ("name", "last_w", "readers", "multi", "ws")

    def __init__(self, name, multi=False):
        self.name = name
        self.last_w = None
        self.readers = []
        self.multi = multi
        self.ws = []


class Op:
    __slots__ = ("eng", "emit", "idx", "is_dma", "waits", "signal", "sem", "target", "count")

    def __init__(self, eng, emit, is_dma):
        self.eng = eng
        self.emit = emit
        self.is_dma = is_dma
        self.waits = []
        self.signal = False
        self.sem = None
        self.target = None
        self.count = None


ENGS = ("pe", "act", "dve", "pool", "sp")
DMA_K = 12
EPOCH = 12000


class Prog:
    def __init__(self):
        self.ops = {e: [] for e in ENGS}
        self.seen = {e: {f: -1 for f in ENGS} for e in ENGS}
        self.seen_dma = {e: {} for e in ENGS}
        self.ndma = {e: 0 for e in ENGS}

    def _dep(self, x, y, kind):
        e = x.eng
        if y.is_dma:
            key = (y.eng, y.sem)
            if self.seen_dma[e].get(key, 0) >= y.target:
                return
            self.seen_dma[e][key] = y.target
            x.waits.append(("dma", y.eng, y.sem, y.target))
            return
        f = y.eng
        if f == e and not x.is_dma:
            if e == "pe":
                return
        if self.seen[e][f] >= y.idx:
            return
        self.seen[e][f] = y.idx
        y.signal = True
        x.waits.append(("eng", y))

    def add(self, eng, emit, reads=(), writes=(), dma=False):
        x = Op(eng, emit, dma)
        x.idx = len(self.ops[eng])
        if dma:
            i = self.ndma[eng]
            self.ndma[eng] += 1
            x.sem = i % DMA_K
            x.target = 16 * (i // DMA_K + 1)
            if i >= DMA_K:
                key = (eng, x.sem)
                if self.seen_dma[eng].get(key, 0) < x.target - 16:
                    self.seen_dma[eng][key] = x.target - 16
                    x.waits.append(("dma", eng, x.sem, x.target - 16))
        for t in reads:
            if t.last_w is not None:
                self._dep(x, t.last_w, "raw")
            for y in t.ws:
                self._dep(x, y, "raw")
        for t in writes:
            if t.multi:
                continue
            if t.last_w is not None:
                self._dep(x, t.last_w, "waw")
            for r in t.readers:
                if r is not x:
                    self._dep(x, r, "war")
        for t in reads:
            t.readers = [r for r in t.readers if r.is_dma or r.eng != eng or dma] + [x]
        for t in writes:
            if t.multi:
                for r in t.readers:
                    if r is not x:
                        self._dep(x, r, "war")
                t.ws.append(x)
                continue
            t.last_w = x
            t.readers = []
        self.ops[eng].append(x)
        return x

    def barrier(self):
        lasts = [self.ops[e][-1] for e in ENGS if self.ops[e]]
        dmas = []
        for e in ENGS:
            dl = [o for o in self.ops[e] if o.is_dma]
            dmas += dl[-DMA_K:]
        for e in ENGS:
            x = Op(e, None, False)
            x.idx = len(self.ops[e])
            for y in lasts:
                if y.is_dma or y.emit is None:
                    continue
                if y.eng == e and e == "pe":
                    continue
                self._dep(x, y, "raw")
            for y in dmas:
                self._dep(x, y, "raw")
            self.ops[e].append(x)

    def emit_all(self, nc):
        import contextlib
        for e in ENGS:
            c = 0
            for o in self.ops[e]:
                if o.signal:
                    c += 1
                    o.count = c
        nep = {e: (sum(1 for o in self.ops[e] if o.signal) // EPOCH + 1) for e in ENGS}
        with contextlib.ExitStack() as st:
            esem = {e: [st.enter_context(nc.semaphore("s_%s_%d" % (e, k))) for k in range(nep[e])]
                    for e in ENGS}
            dsem = {e: [st.enter_context(nc.semaphore("d_%s_%d" % (e, k))) for k in range(DMA_K)]
                    for e in ENGS if self.ndma[e] > 0}
            block = st.enter_context(nc.Block())

            def run(ename):
                def body(eng):
                    for o in self.ops[ename]:
                        for w in o.waits:
                            if w[0] == "dma":
                                eng.wait_ge(dsem[w[1]][w[2]], w[3])
                            else:
                                y = w[1]
                                ep = (y.count - 1) // EPOCH
                                eng.wait_ge(esem[y.eng][ep], y.count - ep * EPOCH)
                        if o.emit is None:
                            continue
                        ins = o.emit(eng)
                        if o.is_dma:
                            ins.then_inc(dsem[ename][o.sem], 16)
                        elif o.signal:
                            ep = (o.count - 1) // EPOCH
                            ins.then_inc(esem[ename][ep], 1)
                return body

            block.tensor(run("pe"))
            block.scalar(run("act"))
            block.vector(run("dve"))
            block.gpsimd(run("pool"))
            block.sync(run("sp"))


class Buf:
    def __init__(self, h, name, nsub=0):
        self.h = h
        self.t = Tile(name)
        self.ts = [Tile("%s_%d" % (name, i)) for i in range(nsub)]

    def __getitem__(self, idx):
        return self.h[idx]


class Scr:
    def __init__(self, nc, name, shape, dtype, kind="Internal"):
        self.ap = nc.dram_tensor(name, list(shape), dtype, kind=kind).ap()
        self.t = Tile(name, multi=True)


DT_SIZE = {F32: 4, BF16: 2}


class KB:
    def __init__(self, nc, T):
        self.nc = nc
        self.T = T
        self.pg = Prog()
        self.sb_off = 16512
        self.uid = 0
        self.rr = {}
        self.ps = [Buf(nc.alloc_psum_tensor("ps%d" % i, [P, 512], F32), "ps%d" % i) for i in range(7)]
        self.psb = Buf(nc.alloc_psum_tensor("psb", [P, 1024], BF16), "psb")

    def sb(self, shape, dtype, name, nsub=0):
        n = 1
        for s in shape[1:]:
            n *= s
        nbytes = (n * DT_SIZE[dtype] + 31) // 32 * 32
        self.uid += 1
        nm = "%s_%d" % (name, self.uid)
        h = self.nc.alloc_sbuf_tensor_at(nm, list(shape), dtype, offset=self.sb_off)
        self.sb_off += nbytes
        assert self.sb_off <= 229344, (name, self.sb_off)
        return Buf(h, nm, nsub)

    def pool(self, n, shape, dtype, name):
        return [self.sb(shape, dtype, "%s%d" % (name, i)) for i in range(n)]

    def nxt(self, key, lst):
        i = self.rr.get(key, 0)
        self.rr[key] = i + 1
        return lst[i % len(lst)]

    def op(self, eng, fn, reads=(), writes=(), dma=False):
        return self.pg.add(eng, fn, [r.t if isinstance(r, Buf) else r for r in reads],
                           [w.t if isinstance(w, Buf) else w for w in writes], dma)

    def dma(self, q, out_ap, in_ap, reads=(), writes=()):
        return self.op(q, lambda e: e.dma_start(out=out_ap, in_=in_ap), reads, writes, dma=True)

    def mm(self, ps, out_ap, lhsT, lhsT_ap, rhs, rhs_ap, start, stop):
        return self.op("pe", lambda e: e.matmul(out_ap, lhsT_ap, rhs_ap, start=start, stop=stop),
                       [lhsT, rhs], [ps])


def fm(ap, lo, hi):
    return ap.rearrange("(c p) t -> p c t", p=P)[:, :, lo:hi]


class Consts:
    pass


def load_consts(kb, d):
    c = Consts()
    nv = d["fvec"].shape[1]
    c.fvec = kb.sb([P, nv], F32, "fvec")
    kb.dma("sp", c.fvec[:], d["fvec"][:, :], writes=[c.fvec])
    c.fvec64 = kb.sb([64, d["fvec64"].shape[1]], F32, "fvec64")
    kb.dma("sp", c.fvec64[:], d["fvec64"][:, :], writes=[c.fvec64])
    c.ones = kb.sb([P, P], BF16, "ones")
    kb.op("dve", lambda e: e.memset(c.ones[:], 1.0), writes=[c.ones])
    T = kb.T
    c.ident = kb.sb([P, P], BF16, "ident")
    kb.dma("pool", c.ident[:], d["ident"][:, :], writes=[c.ident])
    c.permT = kb.sb([P, P], BF16, "permT")
    kb.dma("pool", c.permT[:], d["permT"][:, :], writes=[c.permT])
    c.epsb = {}
    for eps in (1e-6, 1e-5, 64e-5):
        b = kb.sb([P, 1], F32, "eps")
        kb.op("dve", lambda e, b=b, eps=eps: e.memset(b[:], eps), writes=[b])
        c.epsb[eps] = b
    return c


def rms_stats(kb, c, src, nchunk, TB, rstd, sqp, eps, dim):
    ps = kb.nxt("ps_stat", kb.ps[5:7])
    for ch in range(nchunk):
        sq = kb.nxt("sq", sqp)
        kb.op("act", lambda e, sq=sq, ch=ch: e.activation(out=sq[:], in_=src[:, ch, :], func=AF.Square),
              [src], [sq])
        kb.mm(ps, ps[:, :TB], c.ones, c.ones[:], sq, sq[:], ch == 0, ch == nchunk - 1)
    tmp = kb.nxt("rstd_tmp", c.rtmp)
    kb.op("act", lambda e: e.activation(out=tmp[:], in_=ps[:, :TB], func=AF.Sqrt, bias=c.epsb[eps][:, 0:1],
                                        scale=1.0 / dim), [ps, c.epsb[eps]], [tmp])
    kb.op("dve", lambda e: e.reciprocal(out=rstd[:], in_=tmp[:]), [tmp], [rstd])


VEC = {}


def gcol(c, name, ch):
    j = VEC[name] * DC + ch
    return c.fvec[:, j:j + 1]


def apt(x):
    if isinstance(x, Scr):
        return x.ap, [x.t]
    return x, []


def tail_phase(kb, c, d, l, oT, wo_ap, x_in, x_out, gpost_mix, gpre_ffn, gpost_ffn):
    T = kb.T
    oT_ap, oT_t = apt(oT)
    x_in_ap, x_in_t = apt(x_in)
    x_out_ap, x_out_t = apt(x_out)
    TB = min(512, T)
    NTB = T // TB
    base = kb.sb_off
    xt = kb.sb([P, DC, TB], F32, "xt")
    m = kb.sb([P, DC, TB], F32, "m")
    at = kb.sb([P, DC, TB], BF16, "at")
    NH = 2
    FH = FFN // P // NH
    uT = kb.sb([P, FH, TB], BF16, "uT", nsub=FH)
    sqp = kb.pool(4, [P, TB], BF16, "sq")
    c.rtmp = kb.pool(2, [P, TB], F32, "rtmp")
    rstd = kb.sb([P, TB], F32, "rstd")
    tmpp = kb.pool(3, [P, TB], F32, "tmp")
    w4 = kb.pool(4, [P, DC * P], BF16, "w4")
    w8 = kb.pool(3, [P, FH * P], BF16, "w8")
    psm = kb.ps[0:5]

    def postnorm_add(gname):
        rms_stats(kb, c, m, DC, TB, rstd, sqp, 1e-6, D)
        for ch in range(DC):
            tmp = kb.nxt("tmp", tmpp)
            kb.op("dve", lambda e, tmp=tmp, ch=ch: e.scalar_tensor_tensor(
                out=tmp[:], in0=m[:, ch, :], scalar=gcol(c, gname, ch), in1=rstd[:],
                op0=ALU.mult, op1=ALU.mult), [m, rstd, c.fvec], [tmp])
            kb.op("pool", lambda e, tmp=tmp, ch=ch: e.tensor_tensor(
                out=xt[:, ch, :], in0=xt[:, ch, :], in1=tmp[:], op=ALU.add), [xt, tmp], [xt])

    for tb in range(NTB):
        lo, hi = tb * TB, (tb + 1) * TB
        kb.dma("sp", xt[:], fm(x_in_ap, lo, hi), reads=x_in_t, writes=[xt])
        kb.dma("sp", at[:], fm(oT_ap, lo, hi), reads=oT_t, writes=[at])
        for ob in range(DC):
            w = kb.nxt("w4", w4)
            kb.dma("pool", w[:], wo_ap[ob], writes=[w])
            ps = kb.nxt("psm", psm)
            for ch in range(DC):
                kb.mm(ps, ps[:, :TB], w, w[:, ch * P:(ch + 1) * P], at, at[:, ch, :], ch == 0, ch == DC - 1)
            kb.op("act", lambda e, ps=ps, ob=ob: e.copy(out=m[:, ob, :], in_=ps[:, :TB]), [ps], [m])
        postnorm_add(gpost_mix)
        rms_stats(kb, c, xt, DC, TB, rstd, sqp, 1e-6, D)
        for ch in range(DC):
            eng = "dve"
            kb.op(eng, lambda e, ch=ch: e.scalar_tensor_tensor(
                out=at[:, ch, :], in0=xt[:, ch, :], scalar=gcol(c, gpre_ffn, ch), in1=rstd[:],
                op0=ALU.mult, op1=ALU.mult), [xt, rstd, c.fvec], [at])
        for half in range(NH):
            for fb in range(FH):
                w = kb.nxt("w4", w4)
                fi = half * FH + fb
                kb.dma("pool", w[:], d["w1_%d_%d" % (l, fi // 16)][fi % 16], writes=[w])
                ps = kb.nxt("psm", psm)
                for ch in range(DC):
                    kb.mm(ps, ps[:, :TB], w, w[:, ch * P:(ch + 1) * P], at, at[:, ch, :], ch == 0, ch == DC - 1)
                r32 = kb.nxt("tmp", tmpp)
                kb.op("dve", lambda e, ps=ps, r32=r32: e.tensor_scalar(out=r32[:], in0=ps[:, :TB], scalar1=0.0,
                                                                       scalar2=None, op0=ALU.max), [ps], [r32])
                kb.op("act", lambda e, r32=r32, fb=fb: e.activation(
                    out=uT[:, fb, :], in_=r32[:], func=AF.Square), [r32], [uT.ts[fb]])
            for ob in range(DC):
                w = kb.nxt("w8", w8)
                kb.dma("pool", w[:], d["w2_%d_%d" % (l, ob // 4)][ob % 4, :, half * FH * P:(half + 1) * FH * P],
                       writes=[w])
                ps = kb.nxt("psm", psm)
                for fc in range(FH):
                    kb.mm(ps, ps[:, :TB], w, w[:, fc * P:(fc + 1) * P], uT.ts[fc], uT[:, fc, :],
                          fc == 0, fc == FH - 1)
                if half == 0:
                    kb.op("act", lambda e, ps=ps, ob=ob: e.copy(out=m[:, ob, :], in_=ps[:, :TB]), [ps], [m])
                else:
                    kb.op("dve", lambda e, ps=ps, ob=ob: e.tensor_tensor(
                        out=m[:, ob, :], in0=ps[:, :TB], in1=m[:, ob, :], op=ALU.add), [ps, m], [m])
        postnorm_add(gpost_ffn)
        kb.dma("sp", fm(x_out_ap, lo, hi), xt[:], reads=[xt], writes=x_out_t)
    kb.pg.barrier()
    kb.sb_off = base


def blk_lhsT(W, ob_size=P):
    K, N = W.shape
    C = K // P
    OB = N // ob_size
    return np.ascontiguousarray(W.reshape(C, P, OB, ob_size).transpose(2, 1, 0, 3).reshape(OB, P, C * ob_size))


VEC_NAMES = ["g_pre_mix0", "g_post_mix0", "g_pre_ffn0", "g_post_ffn0",
             "g_pre_mix1", "g_post_mix1", "g_pre_ffn1", "g_post_ffn1",
             "mix0", "mix1", "mix2", "mix3", "mix4", "mix5",
             "w0", "a0", "kk", "ka", "rk", "lnx_g", "lnx_b"]
for _i, _n in enumerate(VEC_NAMES):
    VEC[_n] = _i


def make_fvec(inp):
    vs = []
    for l in range(2):
        vs += [inp["g_pre_mix"][l], inp["g_post_mix"][l], inp["g_pre_ffn"][l], inp["g_post_ffn"][l]]
    vs += [inp["rw_mix"][0, i] for i in range(6)]
    vs += [inp["rw_w0"][0], inp["rw_a0"][0], inp["rw_kk"][0], inp["rw_ka"][0], inp["rw_rk"][0].reshape(-1),
           inp["rw_lnx_g"][0], inp["rw_lnx_b"][0]]
    a = np.stack(vs, 0).astype(np.float32)
    nv = a.shape[0]
    return np.ascontiguousarray(a.reshape(nv, DC, P).transpose(2, 0, 1).reshape(P, nv * DC))


def make_fvec64(inp):
    f = make_fvec(inp)
    nv = f.shape[1] // DC
    a = f.reshape(P, nv, DC).transpose(1, 2, 0).reshape(nv, D)
    return np.ascontiguousarray(a.reshape(nv, D // 64, 64).transpose(2, 0, 1).reshape(64, nv * (D // 64)))


def prenorm_block(kb, c, x, lo, hi, xt, hT, rstd, sqp, gname, TB):
    x_ap, x_t = apt(x)
    kb.dma("sp", xt[:], fm(x_ap, lo, hi), reads=x_t, writes=[xt])
    rms_stats(kb, c, xt, DC, TB, rstd, sqp, 1e-6, D)
    for ch in range(DC):
        eng = "dve"
        kb.op(eng, lambda e, ch=ch: e.scalar_tensor_tensor(
            out=hT[:, ch, :], in0=xt[:, ch, :], scalar=gcol(c, gname, ch), in1=rstd[:],
            op0=ALU.mult, op1=ALU.mult), [xt, rstd, c.fvec], [hT])


def attn_proj_phase(kb, c, d, x_ap, qT, kT, V):
    T = kb.T
    TB = min(512, T)
    NTB = T // TB
    base = kb.sb_off
    c.ropeC = kb.sb([P, T], F32, "ropeC")
    kb.dma("sp", c.ropeC[:], d["ropeC"][:, :], writes=[c.ropeC])
    c.ropeS = kb.sb([P, T], F32, "ropeS")
    kb.dma("sp", c.ropeS[:], d["ropeS"][:, :], writes=[c.ropeS])
    xt = kb.sb([P, DC, TB], F32, "xt")
    hT = kb.sb([P, DC, TB], BF16, "hT")
    sqp = kb.pool(4, [P, TB], BF16, "sq")
    c.rtmp = kb.pool(2, [P, TB], F32, "rtmp")
    rstd = kb.sb([P, TB], F32, "rstd")
    w4 = kb.pool(4, [P, DC * P], BF16, "w4")
    w16 = kb.pool(2, [P, DC * 512], BF16, "w16")
    qsb = kb.pool(3, [P, TB], BF16, "qsb")
    t1p = kb.pool(3, [P, TB], F32, "t1")
    t2p = kb.pool(3, [P, TB], F32, "t2")
    qrp = kb.pool(3, [P, TB], BF16, "qr")
    vsb = kb.pool(3, [P, 512], BF16, "vsb")
    psm = kb.ps[0:4]
    psr = kb.ps[4:5]
    for tb in range(NTB):
        lo, hi = tb * TB, (tb + 1) * TB
        prenorm_block(kb, c, x_ap, lo, hi, xt, hT, rstd, sqp, "g_pre_mix0", TB)
        for (wname, dst) in (("wq", qT), ("wk", kT)):
            for s in range(16):
                w = kb.nxt("w4", w4)
                kb.dma("pool", w[:], d[wname][s], writes=[w])
                ps = kb.nxt("psm", psm)
                for ch in range(DC):
                    kb.mm(ps, ps[:, :TB], w, w[:, ch * P:(ch + 1) * P], hT, hT[:, ch, :], ch == 0, ch == DC - 1)
                q = kb.nxt("qsb", qsb)
                kb.op("act", lambda e, q=q, ps=ps: e.copy(out=q[:], in_=ps[:, :TB]), [ps], [q])
                pr = kb.nxt("psr", psr)
                kb.mm(pr, pr[:, :TB], c.permT, c.permT[:], q, q[:], True, True)
                t1 = kb.nxt("t1", t1p)
                t2 = kb.nxt("t2", t2p)
                qr = kb.nxt("qr", qrp)
                rs_, rc_ = c.ropeS[:, lo:hi], c.ropeC[:, lo:hi]
                kb.op("dve", lambda e, t1=t1, pr=pr, rs_=rs_: e.tensor_tensor(
                    out=t1[:], in0=pr[:, :TB], in1=rs_, op=ALU.mult), [pr, c.ropeS], [t1])
                kb.op("pool", lambda e, t2=t2, q=q, rc_=rc_: e.tensor_tensor(
                    out=t2[:], in0=q[:], in1=rc_, op=ALU.mult), [q, c.ropeC], [t2])
                kb.op("pool", lambda e, t1=t1, t2=t2, qr=qr: e.tensor_tensor(
                    out=qr[:], in0=t1[:], in1=t2[:], op=ALU.add), [t1, t2], [qr])
                kb.dma("sp", dst.ap[s, :, lo:hi], qr[:], reads=[qr], writes=[dst.t])
        for nb in range(4):
            w = kb.nxt("w16", w16)
            kb.dma("pool", w[:], d["wv"][nb], writes=[w])
            for ts_ in range(TB // P):
                ps = kb.nxt("psm", psm)
                for ch in range(DC):
                    kb.mm(ps, ps[:, :512], hT, hT[:, ch, ts_ * P:(ts_ + 1) * P], w, w[:, ch * 512:(ch + 1) * 512],
                          ch == 0, ch == DC - 1)
                v = kb.nxt("vsb", vsb)
                kb.op("act", lambda e, v=v, ps=ps: e.copy(out=v[:], in_=ps[:, :512]), [ps], [v])
                kb.dma("sp", V.ap[lo + ts_ * P:lo + (ts_ + 1) * P, nb * 512:(nb + 1) * 512], v[:],
                       reads=[v], writes=[V.t])
    kb.pg.barrier()
    kb.sb_off = base


def attn_core_phase(kb, c, d, qT, kT, V, oT, lambda_init):
    T = kb.T
    GS = min(512, T)
    NG = T // GS
    NB = T // P
    BPG = GS // P
    base = kb.sb_off
    scale = 128.0 ** -0.5
    lv = kb.sb([P, 512], F32, "lv")
    kb.dma("sp", lv[:], d["lamv"][:, :], writes=[lv])
    lp = kb.sb([P, 256], F32, "lp")
    ls = kb.sb([P, 2], F32, "ls")
    le = kb.sb([P, 2], F32, "le")
    negl = kb.sb([P, 1], F32, "negl")
    kb.op("dve", lambda e: e.tensor_tensor(out=lp[:, 0:128], in0=lv[:, 0:128], in1=lv[:, 128:256], op=ALU.mult),
          [lv], [lp])
    kb.op("dve", lambda e: e.tensor_tensor(out=lp[:, 128:256], in0=lv[:, 256:384], in1=lv[:, 384:512], op=ALU.mult),
          [lv], [lp])
    kb.op("dve", lambda e: e.reduce_sum(out=ls[:, 0:1], in_=lp[:, 0:128], axis=AX.X), [lp], [ls])
    kb.op("dve", lambda e: e.reduce_sum(out=ls[:, 1:2], in_=lp[:, 128:256], axis=AX.X), [lp], [ls])
    kb.op("act", lambda e: e.activation(out=le[:], in_=ls[:], func=AF.Exp), [ls], [le])
    kb.op("dve", lambda e: e.tensor_tensor(out=negl[:], in0=le[:, 1:2], in1=le[:, 0:1], op=ALU.subtract), [le], [negl])
    kb.op("dve", lambda e: e.tensor_scalar(out=negl[:], in0=negl[:], scalar1=-lambda_init, scalar2=None, op0=ALU.add),
          [negl], [negl])
    gsub = kb.sb([P, 256], F32, "gsub")
    kb.dma("sp", gsub[:], d["subln"][:, :], writes=[gsub])
    kb.op("dve", lambda e: e.tensor_scalar(out=gsub[:], in0=gsub[:], scalar1=1.0 - lambda_init, scalar2=None,
                                           op0=ALU.mult), [gsub], [gsub])
    kq = [[kb.sb([P, T], BF16, "kT%d%d" % (i, m)) for m in range(2)] for i in range(2)]
    qq = [[kb.sb([P, T], BF16, "qT%d%d" % (i, m)) for m in range(2)] for i in range(2)]
    vx = [kb.sb([P, NB, 257], BF16, "vx%d" % i) for i in range(2)]
    for i in range(2):
        kb.op("dve", lambda e, i=i: e.memset(vx[i][:, :, 256:257], 1.0), writes=[vx[i]])
    esb = kb.pool(4, [P, GS], BF16, "esb")
    o1 = kb.pool(2, [P, BPG, 256], F32, "o1")
    rcp = kb.pool(4, [P, 1], F32, "rcp")
    odp = kb.pool(3, [P, 256], F32, "od")
    sqq = kb.pool(2, [P, 256], F32, "sqq")
    ssp = kb.pool(4, [P, 1], F32, "ss")
    onp = kb.pool(3, [P, 256], BF16, "on")
    oTs = kb.pool(2, [P, 2, GS], BF16, "oTs")
    pso = kb.ps[0:4]
    pss = kb.ps[4:7]
    for h in range(8):
        kk_, qq_, vx_ = kq[h % 2], qq[h % 2], vx[h % 2]
        for m in range(2):
            kb.dma("sp", kk_[m][:], kT.ap[2 * h + m], reads=[kT.t], writes=[kk_[m]])
            kb.dma("sp", qq_[m][:], qT.ap[2 * h + m], reads=[qT.t], writes=[qq_[m]])
        kb.dma("sp", vx_[:, :, 0:256], V.ap.rearrange("(j p) e -> p j e", p=P)[:, :, h * 256:(h + 1) * 256],
               reads=[V.t], writes=[vx_])
        for G in range(NG):
            o1b = kb.nxt("o1", o1)
            oT_sb = kb.nxt("oTs", oTs)
            for m in range(2):
                nj = G * BPG + BPG
                for j in range(nj):
                    il0 = max(0, j - G * BPG)
                    c0 = il0 * P
                    ps = kb.nxt("pss", pss)
                    kb.mm(ps, ps[:, c0:GS], kk_[m], kk_[m][:, j * P:(j + 1) * P], qq_[m],
                          qq_[m][:, G * GS + c0:(G + 1) * GS], True, True)
                    es = kb.nxt("esb", esb)
                    kb.op("act", lambda e, es=es, ps=ps, c0=c0: e.activation(
                        out=es[:, c0:GS], in_=ps[:, c0:GS], func=AF.Exp, scale=scale), [ps], [es])
                    if j >= G * BPG:
                        kb.op("pool", lambda e, es=es, c0=c0: e.memset(es[64:128, c0:c0 + 64], 0.0), [], [es])
                    for il in range(il0, BPG):
                        kb.mm(pso[il], pso[il][:, 0:257], es, es[:, il * P:(il + 1) * P], vx_, vx_[:, j, :],
                              j == 0, j == G * BPG + il)
                for il in range(BPG):
                    po = pso[il]
                    rc = kb.nxt("rcp", rcp)
                    kb.op("dve", lambda e, rc=rc, po=po: e.reciprocal(out=rc[:], in_=po[:, 256:257]), [po], [rc])
                    if m == 0:
                        kb.op("dve", lambda e, rc=rc, po=po, il=il, o1b=o1b: e.tensor_scalar(
                            out=o1b[:, il, :], in0=po[:, 0:256], scalar1=rc[:, 0:1], scalar2=None, op0=ALU.mult),
                            [po, rc], [o1b])
                        continue
                    rc2 = kb.nxt("rcp", rcp)
                    kb.op("dve", lambda e, rc=rc, rc2=rc2: e.tensor_tensor(
                        out=rc2[:], in0=rc[:], in1=negl[:], op=ALU.mult), [rc, negl], [rc2])
                    od = kb.nxt("od", odp)
                    kb.op("dve", lambda e, od=od, po=po, rc2=rc2, il=il, o1b=o1b: e.scalar_tensor_tensor(
                        out=od[:], in0=po[:, 0:256], scalar=rc2[:, 0:1], in1=o1b[:, il, :],
                        op0=ALU.mult, op1=ALU.add), [po, rc2, o1b], [od])
                    sq = kb.nxt("sqq", sqq)
                    ss = kb.nxt("ss", ssp)
                    kb.op("pool", lambda e, sq=sq, od=od: e.tensor_tensor(out=sq[:], in0=od[:], in1=od[:], op=ALU.mult),
                          [od], [sq])
                    kb.op("dve", lambda e, sq=sq, ss=ss: e.reduce_sum(out=ss[:], in_=sq[:], axis=AX.X), [sq], [ss])
                    ss2 = kb.nxt("ss", ssp)
                    kb.op("act", lambda e, ss=ss, ss2=ss2: e.activation(
                        out=ss2[:], in_=ss[:], func=AF.Sqrt, bias=c.epsb[1e-5][:, 0:1], scale=1.0 / 256),
                        [ss, c.epsb[1e-5]], [ss2])
                    ss3 = kb.nxt("ss", ssp)
                    kb.op("dve", lambda e, ss2=ss2, ss3=ss3: e.reciprocal(out=ss3[:], in_=ss2[:]), [ss2], [ss3])
                    on = kb.nxt("on", onp)
                    kb.op("dve", lambda e, on=on, od=od, ss3=ss3: e.scalar_tensor_tensor(
                        out=on[:], in0=od[:], scalar=ss3[:, 0:1], in1=gsub[:], op0=ALU.mult, op1=ALU.mult),
                        [od, ss3, gsub], [on])
                    pb = kb.psb
                    for ec in range(2):
                        kb.op("pe", lambda e, on=on, ec=ec: e.transpose(
                            out=pb[:, ec * P:(ec + 1) * P], in_=on[:, ec * P:(ec + 1) * P], identity=c.ident[:]),
                            [on, c.ident], [pb])
                    kb.op("act", lambda e, oT_sb=oT_sb, il=il: e.copy(
                        out=oT_sb[:, :, il * P:(il + 1) * P], in_=pb[:, 0:256].rearrange("p (a b) -> p a b", a=2)),
                        [pb], [oT_sb])
            kb.dma("sp", oT.ap[h * 256:(h + 1) * 256, G * GS:(G + 1) * GS].rearrange("(a p) t -> p a t", p=P),
                   oT_sb[:], reads=[oT_sb], writes=[oT.t])
    kb.pg.barrier()
    kb.sb_off = base


def host_consts(T):
    out = {}
    out["ident"] = np.eye(P, dtype=np.float32)
    pm = np.zeros((P, P), np.float32)
    for m_ in range(16):
        pm[m_ + 16, m_] = 1.0
        pm[m_, m_ + 16] = 1.0
    out["permT"] = pm
    half = 16
    inv_freq = (np.float32(500000.0) ** (-np.arange(half, dtype=np.float32) * np.float32(2.0 / 32))).astype(np.float32)
    pos = np.arange(T, dtype=np.float32)
    ang = (pos[:, None] * inv_freq[None, :]).astype(np.float32)
    cos = np.cos(ang).astype(np.float32).T
    sin = np.sin(ang).astype(np.float32).T
    C = np.ones((P, T), np.float32)
    S = np.zeros((P, T), np.float32)
    C[0:16] = cos
    C[16:32] = cos
    S[0:16] = -sin
    S[16:32] = sin
    out["ropeC"] = C
    out["ropeS"] = S
    return out


def rwkv_proj_phase(kb, c, d, x_in, S):
    T = kb.T
    TB = min(512, T)
    NTB = T // TB
    base = kb.sb_off
    xt = kb.sb([P, DC, TB], F32, "xt")
    h = kb.sb([P, DC, TB], F32, "h")
    hlast = kb.sb([P, DC, 1], F32, "hlast")
    kb.op("dve", lambda e: e.memset(hlast[:], 0.0), writes=[hlast])
    xmp = kb.pool(2, [P, DC, TB], BF16, "xm")
    sqp = kb.pool(4, [P, TB], BF16, "sq")
    c.rtmp = kb.pool(2, [P, TB], F32, "rtmp")
    rstd = kb.sb([P, TB], F32, "rstd")
    w4 = kb.pool(4, [P, DC * P], BF16, "w4")
    w16 = kb.pool(2, [P, DC * 256], BF16, "w16")
    stg = kb.pool(4, [P, TB], F32, "stg")
    vsb = kb.pool(3, [P, 512], BF16, "vsb")
    t96 = kb.pool(2, [P, TB], BF16, "t96")
    tg = kb.sb([P, 2, TB], BF16, "tg")
    w1l = kb.sb([P, DC * 96], BF16, "w1l")
    a1l = kb.sb([P, DC * 96], BF16, "a1l")
    w2l = kb.sb([P, D], BF16, "w2l")
    a2l = kb.sb([P, D], BF16, "a2l")
    g1l = kb.sb([P, 2, DC * P], BF16, "g1l")
    g2l = kb.sb([P, 2, D], BF16, "g2l")
    kb.dma("pool", w1l[:], d["rw_w1"][0], writes=[w1l])
    kb.dma("pool", a1l[:], d["rw_a1"][0], writes=[a1l])
    kb.dma("pool", w2l[0:96, :], d["rw_w2"][:, :], writes=[w2l])
    kb.dma("pool", a2l[0:96, :], d["rw_a2"][:, :], writes=[a2l])
    kb.dma("pool", g1l[:], d["rw_g1"].rearrange("o p f -> p o f"), writes=[g1l])
    kb.dma("pool", g2l[:], d["rw_g2"].rearrange("(o p) f -> p o f", p=P), writes=[g2l])
    negw0 = kb.sb([P, DC], F32, "negw0")
    j0 = VEC["w0"] * DC
    kb.op("dve", lambda e: e.tensor_scalar(out=negw0[:], in0=c.fvec[:, j0:j0 + DC], scalar1=-1.0, scalar2=None,
                                           op0=ALU.mult), [c.fvec], [negw0])
    one = kb.sb([P, 1], F32, "one")
    nhalf = kb.sb([P, 1], F32, "nhalf")
    kb.op("dve", lambda e: e.memset(one[:], 1.0), writes=[one])
    kb.op("dve", lambda e: e.memset(nhalf[:], -0.5), writes=[nhalf])
    psm = kb.ps[0:5]

    def proj_fm(wap, nob, xm, evac):
        for ob in range(nob):
            w = kb.nxt("w4", w4)
            kb.dma("pool", w[:], wap[ob], writes=[w])
            ps = kb.nxt("psm", psm)
            for ch in range(DC):
                kb.mm(ps, ps[:, :TB], w, w[:, ch * P:(ch + 1) * P], xm, xm[:, ch, :], ch == 0, ch == DC - 1)
            evac(ob, ps)

    for tb in range(NTB):
        lo, hi = tb * TB, (tb + 1) * TB

        def store(ob, st, dst):
            kb.dma("sp", dst.ap[ob * P:(ob + 1) * P, lo:hi], st[:], reads=[st], writes=[dst.t])

        def copy_store(dst):
            def f(ob, ps):
                st = kb.nxt("stg", stg)
                kb.op("act", lambda e, st=st, ps=ps: e.copy(out=st[:], in_=ps[:, :TB]), [ps], [st])
                store(ob, st, dst)
            return f

        prenorm_block(kb, c, x_in, lo, hi, xt, h, rstd, sqp, "g_pre_mix1", TB)
        kb.op("dve", lambda e: e.tensor_tensor(out=xt[:, :, 1:TB], in0=h[:, :, 0:TB - 1], in1=h[:, :, 1:TB],
                                               op=ALU.subtract), [h], [xt])
        kb.op("dve", lambda e: e.tensor_tensor(out=xt[:, :, 0:1], in0=hlast[:], in1=h[:, :, 0:1],
                                               op=ALU.subtract), [h, hlast], [xt])
        kb.op("pool", lambda e: e.tensor_copy(out=hlast[:], in_=h[:, :, TB - 1:TB]), [h], [hlast])

        def mixed(i):
            xm = kb.nxt("xm", xmp)
            for ch in range(DC):
                eng = "dve"
                kb.op(eng, lambda e, ch=ch, xm=xm: e.scalar_tensor_tensor(
                    out=xm[:, ch, :], in0=xt[:, ch, :], scalar=gcol(c, "mix%d" % i, ch), in1=h[:, ch, :],
                    op0=ALU.mult, op1=ALU.add), [xt, h, c.fvec], [xm])
            return xm

        def lora96(xm, wl, func):
            ps = kb.nxt("psm", psm)
            for ch in range(DC):
                kb.mm(ps, ps[0:96, :TB], wl, wl[:, ch * 96:(ch + 1) * 96], xm, xm[:, ch, :], ch == 0, ch == DC - 1)
            t = kb.nxt("t96", t96)
            kb.op("act", lambda e, t=t, ps=ps: e.activation(out=t[0:96, :], in_=ps[0:96, :TB], func=func), [ps], [t])
            return t

        xm = mixed(0)
        proj_fm(d["rw_wr"], DC, xm, copy_store(S["rT"]))
        xm = mixed(1)
        t = lora96(xm, w1l, AF.Tanh)
        for ob in range(DC):
            ps = kb.nxt("psm", psm)
            kb.mm(ps, ps[:, :TB], w2l, w2l[0:96, ob * P:(ob + 1) * P], t, t[0:96, :], True, True)
            s1 = kb.nxt("stg", stg)
            s2 = kb.nxt("stg", stg)
            s3 = kb.nxt("stg", stg)
            kb.op("act", lambda e, s1=s1, ps=ps, ob=ob: e.activation(
                out=s1[:], in_=ps[:, :TB], func=AF.Exp, scale=-1.0, bias=negw0[:, ob:ob + 1]), [ps, negw0], [s1])
            kb.op("act", lambda e, s1=s1, s2=s2: e.activation(
                out=s2[:], in_=s1[:], func=AF.Ln, scale=1.0, bias=one[:, 0:1]), [s1, one], [s2])
            kb.op("act", lambda e, s2=s2, s3=s3: e.activation(
                out=s3[:], in_=s2[:], func=AF.Exp, scale=-1.0, bias=nhalf[:, 0:1]), [s2, nhalf], [s3])
            store(ob, s3, S["ewT"])
        xm = mixed(2)
        proj_fm(d["rw_wk"], DC, xm, copy_store(S["kT"]))
        xm = mixed(3)
        proj_fm(d["rw_wv"], DC, xm, copy_store(S["vT"]))
        for nb in range(8):
            w = kb.nxt("w16", w16)
            kb.dma("pool", w[:], d["rw_wv_r"][nb], writes=[w])
            for ts_ in range(TB // P):
                ps = kb.nxt("psm", psm)
                for ch in range(DC):
                    kb.mm(ps, ps[:, :256], xm, xm[:, ch, ts_ * P:(ts_ + 1) * P], w, w[:, ch * 256:(ch + 1) * 256],
                          ch == 0, ch == DC - 1)
                v = kb.nxt("vsb", vsb)
                kb.op("act", lambda e, v=v, ps=ps: e.copy(out=v[:, :256], in_=ps[:, :256]), [ps], [v])
                kb.dma("sp", S["V"].ap[lo + ts_ * P:lo + (ts_ + 1) * P, nb * 256:(nb + 1) * 256], v[:, :256],
                       reads=[v], writes=[S["V"].t])
        xm = mixed(4)
        t = lora96(xm, a1l, AF.Copy)
        for ob in range(DC):
            ps = kb.nxt("psm", psm)
            kb.mm(ps, ps[:, :TB], a2l, a2l[0:96, ob * P:(ob + 1) * P], t, t[0:96, :], True, True)
            s1 = kb.nxt("stg", stg)
            kb.op("act", lambda e, s1=s1, ps=ps, ob=ob: e.activation(
                out=s1[:], in_=ps[:, :TB], func=AF.Sigmoid, bias=gcol(c, "a0", ob)), [ps, c.fvec], [s1])
            store(ob, s1, S["aT"])
        xm = mixed(5)
        for o2 in range(2):
            ps = kb.nxt("psm", psm)
            for ch in range(DC):
                kb.mm(ps, ps[:, :TB], g1l, g1l[:, o2, ch * P:(ch + 1) * P], xm, xm[:, ch, :], ch == 0, ch == DC - 1)
            kb.op("act", lambda e, ps=ps, o2=o2: e.activation(out=tg[:, o2, :], in_=ps[:, :TB], func=AF.Sigmoid),
                  [ps], [tg])
        for ob in range(DC):
            ps = kb.nxt("psm", psm)
            for o2 in range(2):
                kb.mm(ps, ps[:, :TB], g2l, g2l[:, o2, ob * P:(ob + 1) * P], tg, tg[:, o2, :], o2 == 0, o2 == 1)
            copy_store(S["gT"])(ob, ps)
    kb.pg.barrier()
    kb.sb_off = base


def rwkv_scan_phase(kb, c, d, S, oT2):
    T = kb.T
    L = 64
    NCH = T // L
    HP = 64
    NHH = 1
    W = NHH * 64
    AW = NHH * 256
    GC = 512 // W
    NGR = NCH // GC
    PW = min(512, T)
    NPW = T // PW
    base = kb.sb_off
    f32 = lambda n: kb.sb([HP, T], F32, n)
    b16 = lambda n: kb.sb([HP, T], BF16, n)
    R, Kb, E, Ab, T1, T2, CS, DL = [f32(n) for n in ("R", "Kb", "E", "Ab", "T1", "T2", "CS", "DL")]
    SQ, rb, khb, bhb, abb = [b16(n) for n in ("SQ", "rb", "khb", "bhb", "abb")]
    Vt = kb.sb([64, NCH, HP], BF16, "Vt")
    Kt = kb.sb([64, NCH, HP], BF16, "Kt")
    Bt = kb.sb([64, NCH, HP], BF16, "Bt")
    YT = kb.sb([HP, T], F32, "YT")
    AX = kb.sb([64, NCH, AW], BF16, "AX", nsub=NCH)
    X5 = kb.sb([64, NCH, W], BF16, "X5", nsub=NGR)
    GL = kb.sb([HP, NCH], F32, "GL")
    maskX = kb.sb([64, 512], F32, "maskX")
    maskQ = kb.sb([64, 512], F32, "maskQ")
    idX = kb.sb([64, 512], F32, "idX")
    kb.dma("sp", maskX[:], d["maskX"][:, :], writes=[maskX])
    kb.dma("sp", maskQ[:], d["maskQ4"][:, :], writes=[maskQ])
    kb.dma("sp", idX[:], d["identX4"][:, :], writes=[idX])
    AQp = kb.pool(2, [64, 512], BF16, "AQ")
    Qp = kb.pool(2, [64, 512], BF16, "Qp")
    Xp = kb.pool(3, [64, 512], BF16, "Xp")
    Xtp = kb.pool(3, [64, 512], BF16, "Xtp")
    IXp = kb.pool(2, [64, 512], BF16, "IXp")
    Rp = kb.pool(2, [64, 512], BF16, "Rp")
    S32 = kb.sb([HP, 64], F32, "S32")
    Sb = kb.sb([HP, 64], BF16, "Sb")
    tS = kb.sb([HP, 64], F32, "tS")
    RHb = kb.pool(2, [64, W], BF16, "RHb")
    Ubp = kb.pool(2, [64, W], BF16, "Ub")
    ob16 = kb.sb([HP, T], BF16, "ob16")
    tiny = kb.sb([HP, 1], F32, "tiny")
    kb.op("dve", lambda e: e.memset(tiny[:], 1e-12), writes=[tiny])
    psA = kb.ps[0:3]
    psQ = kb.ps[3:4]
    psI = kb.ps[4:7]

    def col(name, pc):
        j_ = VEC[name] * (D // HP) + pc
        return c.fvec64[:, j_:j_ + 1]

    def view3(buf, lo_chunk=0, hi_chunk=None):
        hi_chunk = NCH if hi_chunk is None else hi_chunk
        return buf[:, lo_chunk * L:hi_chunk * L].rearrange("p (n l) -> p n l", l=L)

    bd = c.ones
    for pc in range(D // HP):
        rows = slice(pc * HP, (pc + 1) * HP)
        kb.dma("sp", R[:], S["rT"].ap[rows, :], reads=[S["rT"].t], writes=[R])
        kb.dma("sp", Kb[:], S["kT"].ap[rows, :], reads=[S["kT"].t], writes=[Kb])
        kb.dma("sp", E[:], S["ewT"].ap[rows, :], reads=[S["ewT"].t], writes=[E])
        kb.dma("sp", Ab[:], S["aT"].ap[rows, :], reads=[S["aT"].t], writes=[Ab])
        kb.dma("sp", Vt[:], S["V"].ap.rearrange("(n p) e -> p n e", p=L)[:, :, pc * HP:(pc + 1) * HP],
               reads=[S["V"].t], writes=[Vt])
        kb.op("dve", lambda e, pc=pc: e.tensor_scalar(out=T1[:], in0=Kb[:], scalar1=col("kk", pc), scalar2=None,
                                                      op0=ALU.mult), [Kb, c.fvec], [T1])
        kb.op("pool", lambda e: e.tensor_tensor(out=SQ[:], in0=T1[:], in1=T1[:], op=ALU.mult), [T1], [SQ])
        for pw in range(NPW):
            ps = kb.nxt("psA", psA)
            cs_ = slice(pw * PW, (pw + 1) * PW)
            kb.mm(ps, ps[0:HP, :PW], bd, bd[0:HP, 0:HP], SQ, SQ[:, cs_], True, True)
            kb.op("act", lambda e, ps=ps, cs_=cs_: e.activation(out=T2[:, cs_], in_=ps[0:HP, :PW], func=AF.Sqrt),
                  [ps], [T2])
        kb.op("dve", lambda e: e.tensor_scalar(out=T2[:], in0=T2[:], scalar1=tiny[:, 0:1], scalar2=None, op0=ALU.max),
              [T2, tiny], [T2])
        kb.op("dve", lambda e: e.reciprocal(out=T2[:], in_=T2[:]), [T2], [T2])
        kb.op("pool", lambda e: e.tensor_tensor(out=T1[:], in0=T1[:], in1=T2[:], op=ALU.mult), [T1, T2], [T1])
        kb.op("dve", lambda e: e.tensor_tensor(out=T2[:], in0=T1[:], in1=Ab[:], op=ALU.mult), [T1, Ab], [T2])
        kb.op("dve", lambda e, pc=pc: e.tensor_scalar(out=Ab[:], in0=Ab[:], scalar1=col("ka", pc), scalar2=col("ka", pc),
                                                      op0=ALU.mult, op1=ALU.subtract), [Ab, c.fvec], [Ab])
        kb.op("dve", lambda e: e.scalar_tensor_tensor(out=Kb[:], in0=Ab[:], scalar=1.0, in1=Kb[:],
                                                       op0=ALU.add, op1=ALU.mult), [Ab, Kb], [Kb])
        bufs = [E, CS, DL, CS, DL, CS, DL]
        for si, sh in enumerate((1, 2, 4, 8, 16, 32)):
            src, dst = bufs[si], bufs[si + 1]
            kb.op("dve", lambda e, src=src, dst=dst, sh=sh: e.tensor_tensor(
                out=view3(dst)[:, :, sh:L], in0=view3(src)[:, :, sh:L], in1=view3(src)[:, :, 0:L - sh], op=ALU.add),
                [src], [dst])
            kb.op("pool", lambda e, src=src, dst=dst, sh=sh: e.tensor_copy(
                out=view3(dst)[:, :, 0:sh], in_=view3(src)[:, :, 0:sh]), [src], [dst])
        kb.op("act", lambda e: e.activation(out=CS[:], in_=DL[:], func=AF.Exp, scale=-1.0), [DL], [CS])
        kb.op("act", lambda e: e.activation(out=Ab[:], in_=DL[:], func=AF.Exp), [DL], [Ab])
        kb.op("dve", lambda e: e.tensor_tensor(out=DL[:], in0=DL[:], in1=E[:], op=ALU.subtract), [DL, E], [DL])
        kb.op("act", lambda e: e.activation(out=E[:], in_=DL[:], func=AF.Exp, scale=-1.0), [DL], [E])
        kb.op("dve", lambda e: e.tensor_copy(out=GL[:], in_=view3(CS)[:, :, L - 1]), [CS], [GL])
        kb.op("dve", lambda e: e.tensor_tensor(out=rb[:], in0=R[:], in1=CS[:], op=ALU.mult), [R, CS], [rb])
        kb.op("pool", lambda e: e.tensor_tensor(out=khb[:], in0=Kb[:], in1=Ab[:], op=ALU.mult), [Kb, Ab], [khb])
        kb.op("dve", lambda e: e.tensor_tensor(out=bhb[:], in0=T2[:], in1=Ab[:], op=ALU.mult), [T2, Ab], [bhb])
        kb.op("dve", lambda e: e.scalar_tensor_tensor(out=abb[:], in0=T1[:], scalar=-1.0, in1=E[:],
                                                       op0=ALU.mult, op1=ALU.mult), [T1, E], [abb])
        import os as _os
        _stop = int(_os.environ.get("SCAN_STOP", "9"))
        if _stop < 1:
            continue
        for (src, dst) in ((khb, Kt), (bhb, Bt)):
            for n in range(NCH):
                pt = kb.nxt("psA", psA)
                kb.mm(pt, pt[0:64, 0:HP], src, src[:, n * L:(n + 1) * L], c.ident, c.ident[0:HP, 0:HP], True, True)
                if n % 2 == 0:
                    kb.op("act", lambda e, dst=dst, n=n, pt=pt: e.copy(out=dst[:, n, :], in_=pt[0:64, 0:HP]), [pt], [dst])
                else:
                    kb.op("dve", lambda e, dst=dst, n=n, pt=pt: e.tensor_copy(out=dst[:, n, :], in_=pt[0:64, 0:HP]),
                          [pt], [dst])
        if _stop < 2:
            continue
        for g in range(NGR):
            n0 = g * GC
            pq = kb.nxt("psQ", psQ)
            for j in range(GC):
                n = n0 + j
                tk = slice(n * L, (n + 1) * L)
                pa = kb.nxt("psA", psA)
                for hh in range(NHH):
                    hs = slice(hh * 64, (hh + 1) * 64)
                    o = hh * 256
                    kb.mm(pa, pa[0:64, o + 0:o + 64], khb, khb[hs, tk], abb, abb[hs, tk], True, True)
                    kb.mm(pa, pa[0:64, o + 64:o + 128], khb, khb[hs, tk], rb, rb[hs, tk], True, True)
                    kb.mm(pa, pa[0:64, o + 128:o + 192], bhb, bhb[hs, tk], abb, abb[hs, tk], True, True)
                    kb.mm(pa, pa[0:64, o + 192:o + 256], bhb, bhb[hs, tk], rb, rb[hs, tk], True, True)
                    kb.mm(pq, pq[0:64, j * W + hh * 64:j * W + hh * 64 + 64], abb, abb[hs, tk], bhb, bhb[hs, tk],
                          True, True)
                kb.op("dve", lambda e, pa=pa, n=n: e.tensor_tensor(out=AX[:, n, :], in0=pa[0:64, 0:AW],
                                                                   in1=maskX[:, 0:AW], op=ALU.mult),
                      [pa, maskX], [AX.ts[n]])
            Qc = kb.nxt("Qp", Qp)
            kb.op("dve", lambda e, pq=pq, Qc=Qc: e.tensor_tensor(out=Qc[:], in0=pq[0:64, :], in1=maskQ[:], op=ALU.mult),
                  [pq, maskQ], [Qc])
            axg = AX[:, n0:n0 + GC, :].rearrange("p n (h k l) -> p n h k l", h=NHH, k=4)[:, :, :, 2, :]
            v4 = lambda b: b[:].rearrange("p (n h l) -> p n h l", n=GC, h=NHH)
            id4 = idX[:].rearrange("p (n h l) -> p n h l", n=GC, h=NHH)
            Xc = kb.nxt("Xp", Xp)
            Xtc = kb.nxt("Xtp", Xtp)
            kb.op("pool", lambda e, Xc=Xc, axg=axg: e.tensor_tensor(out=v4(Xc), in0=axg, in1=id4, op=ALU.add),
                  [AX.ts[n0 + j] for j in range(GC)] + [idX], [Xc])
            kb.op("pool", lambda e, Xtc=Xtc, Qc=Qc: e.tensor_tensor(out=Xtc[:], in0=Qc[:], in1=idX[:], op=ALU.add),
                  [Qc, idX], [Xtc])
            NIT = 5
            for it in range(NIT):
                last = it == NIT - 1
                IX = kb.nxt("IXp", IXp)
                kb.op("pool", lambda e, IX=IX, Xc=Xc: e.tensor_tensor(out=IX[:], in0=idX[:], in1=Xc[:], op=ALU.subtract),
                      [idX, Xc], [IX])
                pR = kb.nxt("psI", psI)
                for j in range(GC):
                    for hh in range(NHH):
                        cs_ = slice(j * W + hh * 64, j * W + hh * 64 + 64)
                        kb.mm(pR, pR[0:64, cs_], Qc, Qc[:, cs_], Xc, Xc[:, cs_], True, True)
                Rc = kb.nxt("Rp", Rp)
                kb.op("dve", lambda e, pR=pR, IX=IX, Rc=Rc: e.tensor_tensor(out=Rc[:], in0=pR[0:64, :], in1=IX[:],
                                                                           op=ALU.add), [pR, IX], [Rc])
                pX = kb.nxt("psI", psI)
                pXt = None if last else kb.nxt("psI", psI)
                for j in range(GC):
                    for hh in range(NHH):
                        cs_ = slice(j * W + hh * 64, j * W + hh * 64 + 64)
                        kb.mm(pX, pX[0:64, cs_], Xtc, Xtc[:, cs_], Rc, Rc[:, cs_], True, False)
                        kb.mm(pX, pX[0:64, cs_], c.ident, c.ident[0:64, 0:64], Xc, Xc[:, cs_], False, True)
                        if not last:
                            kb.mm(pXt, pXt[0:64, cs_], Rc, Rc[:, cs_], Xtc, Xtc[:, cs_], True, False)
                            kb.mm(pXt, pXt[0:64, cs_], c.ident, c.ident[0:64, 0:64], Xtc, Xtc[:, cs_], False, True)
                if last:
                    dstX = X5[:, n0:n0 + GC, :].rearrange("p n e -> p (n e)")
                    kb.op("act", lambda e, pX=pX, dstX=dstX: e.copy(out=dstX, in_=pX[0:64, :]), [pX], [X5.ts[g]])
                else:
                    Xn = kb.nxt("Xp", Xp)
                    Xtn = kb.nxt("Xtp", Xtp)
                    kb.op("act", lambda e, pX=pX, Xn=Xn: e.copy(out=Xn[:], in_=pX[0:64, :]), [pX], [Xn])
                    kb.op("act", lambda e, pXt=pXt, Xtn=Xtn: e.copy(out=Xtn[:], in_=pXt[0:64, :]), [pXt], [Xtn])
                    Xc, Xtc = Xn, Xtn
        if _stop < 3:
            continue
        kb.op("dve", lambda e: e.memset(S32[:], 0.0), writes=[S32])
        kb.op("pool", lambda e: e.memset(Sb[:], 0.0), writes=[Sb])
        psR, psU, psY, psS = kb.ps[0], kb.ps[1], kb.ps[2], kb.ps[3]
        for n in range(NCH):
            tk = slice(n * L, (n + 1) * L)
            axt = AX.ts[n]
            for hh in range(NHH):
                hs = slice(hh * 64, (hh + 1) * 64)
                o = hh * 256
                kb.mm(psR, psR[0:64, hs], abb, abb[hs, tk], Sb, Sb[hs, :], True, False)
                kb.mm(psR, psR[0:64, hs], axt, AX[:, n, o:o + 64], Vt, Vt[:, n, hs], False, True)
            RH = kb.nxt("RHb", RHb)
            kb.op("act", lambda e, RH=RH: e.copy(out=RH[:], in_=psR[0:64, 0:W]), [psR], [RH])
            for hh in range(NHH):
                hs = slice(hh * 64, (hh + 1) * 64)
                kb.mm(psU, psU[0:64, hs], X5.ts[n // GC], X5[:, n, hs], RH, RH[:, hs], True, True)
            Ub = kb.nxt("Ub", Ubp)
            kb.op("dve", lambda e, Ub=Ub: e.tensor_copy(out=Ub[:], in_=psU[0:64, 0:W]), [psU], [Ub])
            for hh in range(NHH):
                hs = slice(hh * 64, (hh + 1) * 64)
                o = hh * 256
                kb.mm(psS, psS[hs, 0:64], Kt, Kt[:, n, hs], Vt, Vt[:, n, hs], True, False)
                kb.mm(psS, psS[hs, 0:64], Bt, Bt[:, n, hs], Ub, Ub[:, hs], False, True)
            for hh in range(NHH):
                hs = slice(hh * 64, (hh + 1) * 64)
                o = hh * 256
                kb.mm(psY, psY[hs, 0:64], Sb, Sb[hs, :], rb, rb[hs, tk], True, False)
                kb.mm(psY, psY[hs, 0:64], Vt, Vt[:, n, hs], axt, AX[:, n, o + 64:o + 128], False, False)
                kb.mm(psY, psY[hs, 0:64], Ub, Ub[:, hs], axt, AX[:, n, o + 192:o + 256], False, True)
            kb.op("dve", lambda e: e.tensor_tensor(out=tS[:], in0=psS[0:HP, 0:64], in1=S32[:], op=ALU.add), [psS, S32], [tS])
            kb.op("dve", lambda e, n=n: e.tensor_scalar(out=S32[:], in0=tS[:], scalar1=GL[:, n:n + 1], scalar2=None,
                                                        op0=ALU.mult), [tS, GL], [S32])
            kb.op("act", lambda e: e.copy(out=Sb[:], in_=S32[:]), [S32], [Sb])
            kb.op("act", lambda e, tk=tk: e.copy(out=YT[:, tk], in_=psY[0:HP, 0:64]), [psY], [YT])
        if _stop < 4:
            continue
        VT, GT, MU, W1_, W2_ = E, Ab, T1, T2, CS
        kb.dma("sp", VT[:], S["vT"].ap[rows, :], reads=[S["vT"].t], writes=[VT])
        kb.dma("sp", GT[:], S["gT"].ap[rows, :], reads=[S["gT"].t], writes=[GT])
        kb.op("act", lambda e: e.copy(out=SQ[:], in_=YT[:]), [YT], [SQ])
        kb.op("pool", lambda e: e.tensor_tensor(out=rb[:], in0=YT[:], in1=YT[:], op=ALU.mult), [YT], [rb])
        kb.op("dve", lambda e, pc=pc: e.scalar_tensor_tensor(out=khb[:], in0=R[:], scalar=col("rk", pc), in1=Kb[:],
                                                             op0=ALU.mult, op1=ALU.mult), [R, Kb, c.fvec], [khb])
        for pw in range(NPW):
            cs_ = slice(pw * PW, (pw + 1) * PW)
            p1 = kb.nxt("psA", psA)
            kb.mm(p1, p1[0:HP, :PW], bd, bd[0:HP, 0:HP], SQ, SQ[:, cs_], True, True)
            kb.op("act", lambda e, p1=p1, cs_=cs_: e.activation(out=MU[:, cs_], in_=p1[0:HP, :PW], func=AF.Copy,
                                                                scale=1.0 / 64), [p1], [MU])
            kb.op("pool", lambda e, cs_=cs_: e.tensor_tensor(out=W1_[:, cs_], in0=MU[:, cs_], in1=MU[:, cs_],
                                                            op=ALU.mult), [MU], [W1_])
            p2 = kb.nxt("psA", psA)
            kb.mm(p2, p2[0:HP, :PW], bd, bd[0:HP, 0:HP], rb, rb[:, cs_], True, True)
            kb.op("dve", lambda e, p2=p2, cs_=cs_: e.scalar_tensor_tensor(
                out=W2_[:, cs_], in0=p2[0:HP, :PW], scalar=1.0 / 64, in1=W1_[:, cs_], op0=ALU.mult, op1=ALU.subtract),
                [p2, W1_], [W2_])
            p3 = kb.nxt("psA", psA)
            kb.mm(p3, p3[0:HP, :PW], bd, bd[0:HP, 0:HP], khb, khb[:, cs_], True, True)
            kb.op("dve", lambda e, p3=p3, cs_=cs_: e.tensor_tensor(out=DL[:, cs_], in0=p3[0:HP, :PW], in1=VT[:, cs_],
                                                                   op=ALU.mult), [p3, VT], [DL])
        kb.op("act", lambda e: e.activation(out=W1_[:], in_=W2_[:], func=AF.Sqrt, bias=c.epsb[64e-5][0:HP, 0:1]),
              [W2_, c.epsb[64e-5]], [W1_])
        kb.op("dve", lambda e: e.reciprocal(out=W1_[:], in_=W1_[:]), [W1_], [W1_])
        kb.op("pool", lambda e: e.tensor_tensor(out=YT[:], in0=YT[:], in1=MU[:], op=ALU.subtract), [YT, MU], [YT])
        kb.op("dve", lambda e: e.tensor_tensor(out=YT[:], in0=YT[:], in1=W1_[:], op=ALU.mult), [YT, W1_], [YT])
        kb.op("dve", lambda e, pc=pc: e.tensor_scalar(out=YT[:], in0=YT[:], scalar1=col("lnx_g", pc),
                                                      scalar2=col("lnx_b", pc), op0=ALU.mult, op1=ALU.add),
              [YT, c.fvec], [YT])
        kb.op("pool", lambda e: e.tensor_tensor(out=YT[:], in0=YT[:], in1=DL[:], op=ALU.add), [YT, DL], [YT])
        kb.op("dve", lambda e: e.tensor_tensor(out=ob16[:], in0=YT[:], in1=GT[:], op=ALU.mult), [YT, GT], [ob16])
        kb.dma("sp", oT2.ap[rows, :], ob16[:], reads=[ob16], writes=[oT2.t])
    kb.pg.barrier()
    kb.sb_off = base


def rwkv_consts():
    out = {}
    j = np.arange(64)[:, None]
    s = np.arange(64)[None, :]
    strict = (j < s).astype(np.float32)
    incl = (j <= s).astype(np.float32)
    out["maskX"] = np.ascontiguousarray(np.concatenate([strict, incl, strict, incl] * 2, 1))
    lowq = (s < j).astype(np.float32)
    out["maskQ4"] = np.ascontiguousarray(np.tile(lowq, (1, 8)))
    out["identX4"] = np.ascontiguousarray(np.tile(np.eye(64, dtype=np.float32), (1, 8)))
    bdm = np.zeros((P, P), np.float32)
    bdm[0:64, 0:64] = 1.0
    bdm[64:128, 64:128] = 1.0
    out["bdones"] = bdm
    return out


def host_inputs(inp, T):
    im = {}
    im["fvec"] = make_fvec(inp)
    im["fvec64"] = make_fvec64(inp)
    im["wq"] = blk_lhsT(np.asarray(inp["da_wq"][0]))
    im["wk"] = blk_lhsT(np.asarray(inp["da_wk"][0]))
    im["wv"] = blk_lhsT(np.asarray(inp["da_wv"][0]), 512)
    im["wo"] = blk_lhsT(np.asarray(inp["da_wo"][0]))
    for l in range(2):
        b1 = blk_lhsT(np.asarray(inp["ffn_w1"][l]))
        b2 = blk_lhsT(np.asarray(inp["ffn_w2"][l]))
        for k in range(4):
            im["w1_%d_%d" % (l, k)] = b1[k * 16:(k + 1) * 16]
            im["w2_%d_%d" % (l, k)] = b2[k * 4:(k + 1) * 4]
    im["lamv"] = np.ascontiguousarray(np.broadcast_to(np.asarray(inp["da_lambda"][0]).reshape(1, 512), (P, 512)))
    im["subln"] = np.ascontiguousarray(np.broadcast_to(np.asarray(inp["da_subln"][0]).reshape(1, 256), (P, 256)))
    im["rw_wr"] = blk_lhsT(np.asarray(inp["rw_wr"][0]))
    im["rw_wk"] = blk_lhsT(np.asarray(inp["rw_wk"][0]))
    im["rw_wv"] = blk_lhsT(np.asarray(inp["rw_wv"][0]))
    im["rw_wv_r"] = blk_lhsT(np.asarray(inp["rw_wv"][0]), 256)
    im["rw_wo"] = blk_lhsT(np.asarray(inp["rw_wo"][0]))
    im["rw_w1"] = blk_lhsT(np.asarray(inp["rw_w1"][0]), 96)
    im["rw_a1"] = blk_lhsT(np.asarray(inp["rw_a1"][0]), 96)
    im["rw_w2"] = np.ascontiguousarray(inp["rw_w2"][0])
    im["rw_a2"] = np.ascontiguousarray(inp["rw_a2"][0])
    im["rw_g1"] = blk_lhsT(np.asarray(inp["rw_g1"][0]))
    im["rw_g2"] = np.ascontiguousarray(inp["rw_g2"][0])
    im.update(host_consts(T))
    im.update(rwkv_consts())
    return {k: np.ascontiguousarray(v, dtype=np.float32) for k, v in im.items()}


def build_full(T, shapes):
    nc = bass.Bass("TRN2", target_bir_lowering=False)
    d = {}
    d["xT"] = nc.dram_tensor("xT", [D, T], F32, kind="ExternalInput").ap()
    for k, shp in shapes.items():
        d[k] = nc.dram_tensor(k, list(shp), F32, kind="ExternalInput").ap()
    d["yT"] = nc.dram_tensor("yT", [D, T], F32, kind="ExternalOutput").ap()
    kb = KB(nc, T)
    c = load_consts(kb, d)
    qT = Scr(nc, "qT", [16, P, T], BF16)
    kT = Scr(nc, "kT", [16, P, T], BF16)
    V = Scr(nc, "V", [T, D], BF16)
    oT = Scr(nc, "oT", [D, T], BF16)
    x1T = Scr(nc, "x1T", [D, T], F32)
    attn_proj_phase(kb, c, d, d["xT"], qT, kT, V)
    attn_core_phase(kb, c, d, qT, kT, V, oT, 0.2)
    tail_phase(kb, c, d, 0, oT, d["wo"], d["xT"], x1T, "g_post_mix0", "g_pre_ffn0", "g_post_ffn0")
    S = {n: Scr(nc, "s_" + n, [D, T], F32) for n in ("rT", "kT", "ewT", "aT", "gT", "vT")}
    S["V"] = Scr(nc, "V2", [T, D], BF16)
    oT2 = Scr(nc, "oT2", [D, T], BF16)
    rwkv_proj_phase(kb, c, d, x1T, S)
    rwkv_scan_phase(kb, c, d, S, oT2)
    tail_phase(kb, c, d, 1, oT2, d["rw_wo"], x1T, d["yT"], "g_post_mix1", "g_pre_ffn1", "g_post_ffn1")
    kb.pg.emit_all(nc)
    return nc, kb


def kernel(**inputs):
    x = np.asarray(inputs["x"], dtype=np.float32)
    B, T, _ = x.shape
    im = host_inputs(inputs, T)
    nc, _ = build_full(T, {k: v.shape for k, v in im.items()})
    in_maps = []
    for b in range(B):
        m = dict(im)
        m["xT"] = np.ascontiguousarray(x[b].T)
        in_maps.append(m)
    res = run_bass_kernel_spmd(nc, in_maps, core_ids=list(range(B)))
    out = np.stack([np.ascontiguousarray(np.asarray(r["yT"]).T) for r in res.results], 0)
    return out.astype(np.float32)
```
